# Optimizing a Trainium2 kernel written in Bass

```python
import math
import jax, jax.numpy as jnp
from jax import lax
import numpy as np

D_MODEL = 1024
BATCH = 8
SEQ = 4096
DEPTH = 4

N_HYB = (DEPTH + 1) // 2
N_REC = DEPTH // 2
NORM_EPS = 1e-6
CONV_K = 4
D_FF = 2816
D_SSM = D_MODEL
SSM_HEAD_DIM = 64
SSM_HEADS = D_SSM // SSM_HEAD_DIM
SSM_GROUPS = 2
SSM_STATE = 128
SSM_CONV_DIM = D_SSM + 2 * SSM_GROUPS * SSM_STATE
SSD_CHUNK = 128
FOX_HEAD_DIM = 128
FOX_HEADS = 8
D_FOX = FOX_HEADS * FOX_HEAD_DIM
Q_BLOCK = 128
HYB_IN = D_SSM + SSM_CONV_DIM + SSM_HEADS + 3 * D_FOX + FOX_HEADS
D_RNN = D_MODEL
RNN_BLOCKS = 8
RNN_BLOCK = D_RNN // RNN_BLOCKS
RG_LRU_C = 8.0

kernel_name = "hybrid_ssd_fox_rglru_macaron_sandwich"


def rms_norm(x, g):
    xf = x.astype(jnp.float32)
    y = xf * lax.rsqrt(jnp.mean(xf * xf, axis=-1, keepdims=True) + NORM_EPS)
    return (y * g.astype(jnp.float32)).astype(x.dtype)


def swiglu(x, w_in, w_out):
    gate, up = jnp.split(x @ w_in, 2, axis=-1)
    return (jax.nn.silu(gate) * up) @ w_out


def causal_conv(x, w, bias):
    s = x.shape[1]
    kw = w.shape[0]
    xp = jnp.pad(x, ((0, 0), (kw - 1, 0), (0, 0)))
    y = xp[:, 0:s] * w[0]
    for k in range(1, kw):
        y = y + xp[:, k:k + s] * w[k]
    return y + bias


def ssd_chunked(xh, dt, a, bm, cm):
    b, s, nh, p = xh.shape
    g, n = bm.shape[-2:]
    kh = nh // g
    c = s // SSD_CHUNK
    X = (xh.astype(jnp.float32) * dt[..., None]).reshape(b, c, SSD_CHUNK, g, kh, p)
    Bc = bm.astype(jnp.float32).reshape(b, c, SSD_CHUNK, g, n)
    Cc = cm.astype(jnp.float32).reshape(b, c, SSD_CHUNK, g, n)
    dA = (dt * a).reshape(b, c, SSD_CHUNK, g, kh).transpose(0, 3, 4, 1, 2)
    a_cs = jnp.cumsum(dA, axis=-1)
    causal = jnp.tril(jnp.ones((SSD_CHUNK, SSD_CHUNK), dtype=bool))
    seg = a_cs[..., :, None] - a_cs[..., None, :]
    L = jnp.exp(jnp.where(causal, seg, -jnp.inf))
    cb = jnp.einsum("bclgn,bcsgn->bgcls", Cc, Bc)
    y_diag = jnp.einsum("bgkcls,bcsgkp->bclgkp", L * cb[:, :, None], X)
    decay_states = jnp.exp(a_cs[..., -1:] - a_cs)
    states = jnp.einsum("bclgn,bgkcl,bclgkp->bcgkpn", Bc, decay_states, X)
    chunk_tot = a_cs[..., -1]
    chunk_cs = jnp.cumsum(chunk_tot, axis=-1)
    cs_prev = chunk_cs - chunk_tot
    seg_c = cs_prev[..., :, None] - chunk_cs[..., None, :]
    strict = jnp.tril(jnp.ones((c, c), dtype=bool), k=-1)
    decay_chunk = jnp.exp(jnp.where(strict, seg_c, -jnp.inf))
    init_states = jnp.einsum("bgkzc,bcgkpn->bzgkpn", decay_chunk, states)
    state_decay_out = jnp.exp(a_cs)
    y_off = jnp.einsum("bclgn,bcgkpn,bgkcl->bclgkp", Cc, init_states, state_decay_out)
    return (y_diag + y_off).reshape(b, s, nh, p)


def forgetting_attention(q, k, v, log_f):
    b, s, h, d = q.shape
    nblk = s // Q_BLOCK
    cum = jnp.cumsum(log_f, axis=1).transpose(0, 2, 1)
    qb = q.reshape(b, nblk, Q_BLOCK, h, d).transpose(1, 0, 2, 3, 4)
    cqb = cum.reshape(b, h, nblk, Q_BLOCK).transpose(2, 0, 1, 3)
    kpos = jnp.arange(s)
    scale = d ** -0.5

    def block(args):
        qi, ci, i = args
        logits = jnp.einsum("bqhd,bkhd->bhqk", qi, k).astype(jnp.float32) * scale
        logits = logits + ci[..., :, None] - cum[:, :, None, :]
        qpos = i * Q_BLOCK + jnp.arange(Q_BLOCK)
        mask = kpos[None, :] <= qpos[:, None]
        probs = jax.nn.softmax(jnp.where(mask, logits, -jnp.inf), axis=-1)
        return jnp.einsum("bhqk,bkhd->bqhd", probs.astype(v.dtype), v)

    out = lax.map(block, (qb, cqb, jnp.arange(nblk)))
    return out.transpose(1, 0, 2, 3, 4).reshape(b, s, h, d)


def hybrid_mixer(h, w_in, conv_w, conv_b, dt_bias, a_log, d_skip, ssm_norm_g, fox_b_f, w_out):
    b, s, _ = h.shape
    sizes = (D_SSM, SSM_CONV_DIM, SSM_HEADS, D_FOX, D_FOX, D_FOX)
    offs = np.cumsum(sizes).tolist()
    z, xbc, dt_raw, q, k, v, f_raw = jnp.split(h @ w_in, offs, axis=-1)
    xbc = jax.nn.silu(causal_conv(xbc, conv_w, conv_b))
    xs, bm, cm = jnp.split(xbc, [D_SSM, D_SSM + SSM_GROUPS * SSM_STATE], axis=-1)
    xh = xs.reshape(b, s, SSM_HEADS, SSM_HEAD_DIM)
    dt = jax.nn.softplus(dt_raw.astype(jnp.float32) + dt_bias.astype(jnp.float32))
    a = -jnp.exp(a_log.astype(jnp.float32))
    y = ssd_chunked(xh, dt, a,
                    bm.reshape(b, s, SSM_GROUPS, SSM_STATE),
                    cm.reshape(b, s, SSM_GROUPS, SSM_STATE))
    y = y + d_skip.astype(jnp.float32)[:, None] * xh.astype(jnp.float32)
    y = y.reshape(b, s, D_SSM) * jax.nn.silu(z.astype(jnp.float32))
    yg = y.reshape(b, s, SSM_GROUPS, D_SSM // SSM_GROUPS)
    yg = yg * lax.rsqrt(jnp.mean(yg * yg, axis=-1, keepdims=True) + NORM_EPS)
    y_ssm = (yg.reshape(b, s, D_SSM) * ssm_norm_g.astype(jnp.float32)).astype(h.dtype)
    log_f = jax.nn.log_sigmoid(f_raw.astype(jnp.float32) + fox_b_f.astype(jnp.float32))
    o = forgetting_attention(q.reshape(b, s, FOX_HEADS, FOX_HEAD_DIM),
                             k.reshape(b, s, FOX_HEADS, FOX_HEAD_DIM),
                             v.reshape(b, s, FOX_HEADS, FOX_HEAD_DIM), log_f)
    o = o.reshape(b, s, D_FOX).astype(h.dtype)
    return jnp.concatenate([y_ssm, o], axis=-1) @ w_out


def rg_lru(x, w_a, b_a, w_x, b_x, lam):
    b, s, d = x.shape
    xb = x.reshape(b, s, RNN_BLOCKS, RNN_BLOCK)
    r = jax.nn.sigmoid((jnp.einsum("bsnj,njk->bsnk", xb, w_a).reshape(b, s, d) + b_a).astype(jnp.float32))
    i = jax.nn.sigmoid((jnp.einsum("bsnj,njk->bsnk", xb, w_x).reshape(b, s, d) + b_x).astype(jnp.float32))
    log_a = RG_LRU_C * r * jax.nn.log_sigmoid(lam.astype(jnp.float32))
    a = jnp.exp(log_a)
    u = jnp.sqrt(-jnp.expm1(2.0 * log_a)) * (i * x.astype(jnp.float32))

    def combine(left, right):
        a1, b1 = left
        a2, b2 = right
        return a1 * a2, a2 * b1 + b2

    _, hs = lax.associative_scan(combine, (a, u), axis=1)
    return hs.astype(x.dtype)


def recurrent_mixer(h, w_in, conv_w, conv_b, w_a, b_a, w_x, b_x, lam, w_out):
    gate, xr = jnp.split(h @ w_in, 2, axis=-1)
    gate = jax.nn.gelu(gate, approximate=True)
    xr = causal_conv(xr, conv_w, conv_b)
    return (rg_lru(xr, w_a, b_a, w_x, b_x, lam) * gate) @ w_out


def setup_inputs(seed: int = 0) -> dict:
    key = jax.random.key(seed)
    ks = jax.random.split(key, 24)
    f32 = jnp.float32

    def nrm(k, shape, scale):
        return jax.random.normal(k, shape, f32) * scale

    def unif(k, shape, lo, hi):
        return jax.random.uniform(k, shape, f32, lo, hi)

    x = nrm(ks[0], (BATCH, SEQ, D_MODEL), 1.0)
    norm_g = 1.0 + nrm(ks[1], (DEPTH, 6, D_MODEL), 0.05)
    ffn_w_in = nrm(ks[2], (DEPTH, 2, D_MODEL, 2 * D_FF), D_MODEL ** -0.5)
    ffn_w_out = nrm(ks[3], (DEPTH, 2, D_FF, D_MODEL), D_FF ** -0.5)
    hyb_w_in = nrm(ks[4], (N_HYB, D_MODEL, HYB_IN), D_MODEL ** -0.5)
    ssm_conv_w = nrm(ks[5], (N_HYB, CONV_K, SSM_CONV_DIM), CONV_K ** -0.5)
    ssm_conv_b = nrm(ks[6], (N_HYB, SSM_CONV_DIM), 0.02)
    dt0 = jnp.exp(unif(ks[7], (N_HYB, SSM_HEADS), math.log(1e-3), math.log(1e-1)))
    ssm_dt_bias = dt0 + jnp.log(-jnp.expm1(-dt0))
    ssm_a_log = jnp.log(unif(ks[8], (N_HYB, SSM_HEADS), 1.0, 16.0))
    ssm_d = 1.0 + nrm(ks[9], (N_HYB, SSM_HEADS), 0.1)
    ssm_norm_g = 1.0 + nrm(ks[10], (N_HYB, D_SSM), 0.05)
    fox_b_f = unif(ks[11], (N_HYB, FOX_HEADS), 1.0, 4.0)
    hyb_w_out = nrm(ks[12], (N_HYB, D_SSM + D_FOX, D_MODEL), (D_SSM + D_FOX) ** -0.5)
    rec_w_in = nrm(ks[13], (N_REC, D_MODEL, 2 * D_RNN), D_MODEL ** -0.5)
    rec_conv_w = nrm(ks[14], (N_REC, CONV_K, D_RNN), CONV_K ** -0.5)
    rec_conv_b = nrm(ks[15], (N_REC, D_RNN), 0.02)
    rec_w_a = nrm(ks[16], (N_REC, RNN_BLOCKS, RNN_BLOCK, RNN_BLOCK), RNN_BLOCK ** -0.5)
    rec_b_a = nrm(ks[17], (N_REC, D_RNN), 0.02)
    rec_w_x = nrm(ks[18], (N_REC, RNN_BLOCKS, RNN_BLOCK, RNN_BLOCK), RNN_BLOCK ** -0.5)
    rec_b_x = nrm(ks[19], (N_REC, D_RNN), 0.02)
    a0 = unif(ks[20], (N_REC, D_RNN), 0.9, 0.999)
    s0 = a0 ** (1.0 / RG_LRU_C)
    rec_lambda = jnp.log(s0) - jnp.log1p(-s0)
    rec_w_out = nrm(ks[21], (N_REC, D_RNN, D_MODEL), D_RNN ** -0.5)
    return {"x": x, "norm_g": norm_g, "ffn_w_in": ffn_w_in, "ffn_w_out": ffn_w_out,
            "hyb_w_in": hyb_w_in, "ssm_conv_w": ssm_conv_w, "ssm_conv_b": ssm_conv_b,
            "ssm_dt_bias": ssm_dt_bias, "ssm_a_log": ssm_a_log, "ssm_d": ssm_d,
            "ssm_norm_g": ssm_norm_g, "fox_b_f": fox_b_f, "hyb_w_out": hyb_w_out,
            "rec_w_in": rec_w_in, "rec_conv_w": rec_conv_w, "rec_conv_b": rec_conv_b,
            "rec_w_a": rec_w_a, "rec_b_a": rec_b_a, "rec_w_x": rec_w_x, "rec_b_x": rec_b_x,
            "rec_lambda": rec_lambda, "rec_w_out": rec_w_out}


def reference(x, norm_g, ffn_w_in, ffn_w_out, hyb_w_in, ssm_conv_w, ssm_conv_b, ssm_dt_bias,
              ssm_a_log, ssm_d, ssm_norm_g, fox_b_f, hyb_w_out, rec_w_in, rec_conv_w, rec_conv_b,
              rec_w_a, rec_b_a, rec_w_x, rec_b_x, rec_lambda, rec_w_out):
    for layer in range(DEPTH):
        g = norm_g[layer]
        h = swiglu(rms_norm(x, g[0]), ffn_w_in[layer, 0], ffn_w_out[layer, 0])
        x = x + 0.5 * rms_norm(h, g[1])
        h = rms_norm(x, g[2])
        if layer % 2 == 0:
            i = layer // 2
            h = hybrid_mixer(h, hyb_w_in[i], ssm_conv_w[i], ssm_conv_b[i], ssm_dt_bias[i],
                             ssm_a_log[i], ssm_d[i], ssm_norm_g[i], fox_b_f[i], hyb_w_out[i])
        else:
            j = layer // 2
            h = recurrent_mixer(h, rec_w_in[j], rec_conv_w[j], rec_conv_b[j], rec_w_a[j], rec_b_a[j],
                                rec_w_x[j], rec_b_x[j], rec_lambda[j], rec_w_out[j])
        x = x + rms_norm(h, g[3])
        h = swiglu(rms_norm(x, g[4]), ffn_w_in[layer, 1], ffn_w_out[layer, 1])
        x = x + 0.5 * rms_norm(h, g[5])
    return x
```

```python
import contextlib
import numpy as np
import concourse.bass as bass
import concourse.mybir as mybir
from concourse.bass_utils import run_bass_kernel_spmd

F32 = mybir.dt.float32
BF16 = mybir.dt.bfloat16
AF = mybir.ActivationFunctionType
ALU = mybir.AluOpType

ENGS = ("pe", "act", "dve", "pool", "sp")
ENGATTR = {"pe": "tensor", "act": "scalar", "dve": "vector", "pool": "gpsimd", "sp": "sync"}

D = 1024
S = 4096
TT = 256
NT = S // TT
DFF = 2816
NFC = DFF // 128
EPS = 1e-6


class Buf:
    __slots__ = ("name", "last_w", "readers")

    def __init__(self, name):
        self.name = name
        self.last_w = None
        self.readers = []


class DSem:
    __slots__ = ("name", "sem", "count", "last_op", "cur_group")

    def __init__(self, name, sem):
        self.name = name
        self.sem = sem
        self.count = 0
        self.last_op = None
        self.cur_group = None


class Op:
    __slots__ = ("eng", "fn", "deps", "signal", "clock", "dsem", "group", "is_dma", "fid")

    def __init__(self, eng, fn, fid):
        self.eng = eng
        self.fn = fn
        self.deps = []
        self.signal = False
        self.clock = 0
        self.dsem = None
        self.group = None
        self.is_dma = False
        self.fid = fid


class Prog:
    def __init__(self, n_dsem=40):
        self.nc = bass.Bass("TRN2", target_bir_lowering=False)
        self.gstack = contextlib.ExitStack()
        self.stack = contextlib.ExitStack()
        self.ops = {e: [] for e in ENGS}
        self.fid = 0
        self.esem = {e: self.gstack.enter_context(self.nc.semaphore(f"s_{e}")) for e in ENGS}
        self.eclock = {e: 0 for e in ENGS}
        self.seen = {e: {} for e in ENGS}
        self.dsems = [DSem(f"d{i}", self.gstack.enter_context(self.nc.semaphore(f"d_{i}")))
                      for i in range(n_dsem)]
        self.nbuf = 0
        self.stats = {"waits": 0, "ops": 0}
        self.first_flush = True

    def dram(self, name, shape, dtype, kind="Internal"):
        return self.nc.dram_tensor(name, list(shape), dtype, kind=kind).ap()

    def sbuf(self, name, shape, dtype, glob=False):
        st = self.gstack if glob else self.stack
        return st.enter_context(self.nc.sbuf_tensor(f"{name}_{self.fid}", list(shape), dtype))

    def psum(self, name, shape, dtype):
        return self.gstack.enter_context(self.nc.psum_tensor(name, list(shape), dtype))

    def buf(self, name=None):
        self.nbuf += 1
        return Buf(name or f"b{self.nbuf}")

    def bufs(self, n, name="b"):
        return [self.buf(f"{name}{i}") for i in range(n)]

    def _add_deps(self, op, reads, writes):
        deps = op.deps
        for b in reads:
            if b.last_w is not None:
                deps.append(b.last_w)
        for b in writes:
            if b.last_w is not None:
                deps.append(b.last_w)
            deps.extend(b.readers)
        for b in reads:
            b.readers.append(op)
        for b in writes:
            b.last_w = op
            b.readers = []

    def op(self, eng, fn, reads=(), writes=()):
        o = Op(eng, fn, self.fid)
        self._add_deps(o, reads, writes)
        self.ops[eng].append(o)
        return o

    def dma(self, q, dsem, out, in_, reads=(), writes=(), new_group=True, **kw):
        o = Op(q, lambda e: e.dma_start(out=out, in_=in_, **kw), self.fid)
        o.is_dma = True
        o.dsem = dsem
        self._add_deps(o, reads, writes)
        if new_group or dsem.cur_group is None:
            if dsem.last_op is not None:
                o.deps.append(dsem.last_op)
            dsem.cur_group = [0]
        dsem.count += 16
        dsem.cur_group[0] = dsem.count
        o.group = dsem.cur_group
        dsem.last_op = o
        self.ops[q].append(o)
        return o

    def flush(self, final=False):
        nc = self.nc
        fid = self.fid
        prev_clock = dict(self.eclock)
        prev_dcount = {d: d.count for d in self.dsems}
        for e in ENGS:
            for o in self.ops[e]:
                if o.is_dma:
                    prev_dcount[o.dsem] -= 16
        for e in ENGS:
            for o in self.ops[e]:
                for d in o.deps:
                    if d.fid != fid or d.is_dma:
                        continue
                    if d.eng == o.eng and d.eng == "pe":
                        continue
                    d.signal = True
            for o in reversed(self.ops[e]):
                if not o.is_dma and o.fn is not None:
                    o.signal = True
                    break
        for e in ENGS:
            c = self.eclock[e]
            for o in self.ops[e]:
                if o.is_dma or o.fn is None:
                    o.clock = c
                    continue
                if o.signal:
                    c += 1
                o.clock = c
            self.eclock[e] = c
        stats = self.stats

        def run_engine(ename, eobj):
            seen = self.seen[ename]
            if not self.first_flush:
                for e2 in ENGS:
                    v = prev_clock[e2]
                    if v > seen.get(e2, 0):
                        eobj.wait_ge(self.esem[e2], v)
                        seen[e2] = v
                        stats["waits"] += 1
                for d in self.dsems:
                    v = prev_dcount[d]
                    if v > seen.get(d, 0):
                        eobj.wait_ge(d.sem, v)
                        seen[d] = v
                        stats["waits"] += 1
            for o in self.ops[ename]:
                need = {}
                for d in o.deps:
                    if d.fid != fid:
                        continue
                    if d.is_dma:
                        key = d.dsem
                        val = d.group[0]
                        sem = d.dsem.sem
                    else:
                        if d.eng == ename and ename == "pe":
                            continue
                        key = d.eng
                        val = d.clock
                        sem = self.esem[d.eng]
                    if seen.get(key, 0) >= val:
                        continue
                    if key not in need or need[key][1] < val:
                        need[key] = (sem, val)
                for key, (sem, val) in need.items():
                    seen[key] = val
                    eobj.wait_ge(sem, val)
                    stats["waits"] += 1
                if o.fn is None:
                    continue
                ins = o.fn(eobj)
                stats["ops"] += 1
                if o.is_dma:
                    ins.then_inc(o.dsem.sem, 16)
                elif o.signal:
                    ins.then_inc(self.esem[ename], 1)
            if final and ename == "sp":
                for e2 in ENGS:
                    if e2 == "sp":
                        continue
                    v = self.eclock[e2]
                    if v > seen.get(e2, 0):
                        eobj.wait_ge(self.esem[e2], v)
                        seen[e2] = v
                for d in self.dsems:
                    if d.count > seen.get(d, 0):
                        eobj.wait_ge(d.sem, d.count)
                        seen[d] = d.count

        with nc.Block() as block:
            for ename in ENGS:
                if not self.ops[ename] and not (final and ename == "sp"):
                    continue
                dec = getattr(block, ENGATTR[ename])

                def body(eobj, ename=ename):
                    run_engine(ename, eobj)

                dec(body)
        self.first_flush = False
        self.ops = {e: [] for e in ENGS}
        self.fid += 1
        self.stack.close()
        self.stack = contextlib.ExitStack()


class Ctx:
    def __init__(self, P):
        self.P = P
        self.ps = [P.psum(f"psb{i}", [128, 512], F32) for i in range(8)]
        self.psb = P.bufs(8, "ps")
        self.ones = P.sbuf("ones", [128, 128], BF16, glob=True)
        self.ident = P.sbuf("ident", [128, 128], BF16, glob=True)
        self.bconst = P.buf("const")
        P.op("pool", lambda e: e.memset(self.ones[:], 1.0), writes=[self.bconst])
        P.op("pool", lambda e: e.memset(self.ident[:], 0.0), writes=[self.bconst])
        P.op("pool", lambda e: e.affine_select(
            out=self.ident[:], in_=self.ident[:], pattern=[[-1, 128]], compare_op=ALU.not_equal,
            fill=1.0, base=0, channel_multiplier=1), reads=[self.bconst], writes=[self.bconst])
        self.di = 0

    def dsem(self):
        d = self.P.dsems[self.di % len(self.P.dsems)]
        self.di += 1
        return d


class WLoader:
    def __init__(self, P, C, n=2, width=2048):
        self.P = P
        self.n = n
        self.stg = [P.sbuf(f"wstg{i}", [128, width], F32) for i in range(n)]
        self.b = P.bufs(n, "wstg")
        self.ds = [C.dsem() for _ in range(n)]
        self.i = 0
        self.casters = ("dve", "act", "pool")

    def load(self, src, dst, dst_buf, inner=None):
        P = self.P
        k = self.i % self.n
        eng = self.casters[self.i % len(self.casters)]
        self.i += 1
        ncols = 1
        for s in src.shape[1:]:
            ncols *= s
        sv = self.stg[k][:, 0:ncols]
        if len(src.shape) == 3:
            sv = sv.rearrange("p (a b) -> p a b", a=src.shape[1])
        P.dma("sp", self.ds[k], sv, src, writes=[self.b[k]])
        if eng == "act":
            P.op("act", lambda e: e.copy(out=dst, in_=sv), reads=[self.b[k]], writes=[dst_buf])
        else:
            P.op(eng, lambda e: e.tensor_copy(out=dst, in_=sv), reads=[self.b[k]], writes=[dst_buf])


def rstd_from_sumsq(P, C, sq, bsq, nchunk, ps_idx, rtmp, rstd, brstd, width, inv_n):
    ps = C.ps[ps_idx]
    bps = C.psb[ps_idx]
    for c in range(nchunk):
        P.op("pe", lambda e, c=c: e.matmul(ps[:, 0:width], lhsT=C.ones[:], rhs=sq[:, c, :],
                                           start=(c == 0), stop=(c == nchunk - 1)),
             reads=[bsq, C.bconst], writes=[bps])
    P.op("act", lambda e: e.activation(out=rtmp[:, 0:width], in_=ps[:, 0:width], func=AF.Sqrt,
                                       scale=inv_n, bias=C.epsc[:]),
         reads=[bps, C.bconst], writes=[brstd])
    P.op("dve", lambda e: e.reciprocal(out=rstd[:, 0:width], in_=rtmp[:, 0:width]),
         reads=[brstd], writes=[brstd])


class NormBufs:
    def __init__(self, P, tag, share=None):
        if share is not None:
            self.sq, self.bsq = share.sq, share.bsq
        else:
            self.sq = P.sbuf(f"sq{tag}", [128, 8, TT], BF16)
            self.bsq = P.buf(f"sq{tag}")
        self.rtmp = P.sbuf(f"rtmp{tag}", [128, TT], F32)
        self.rstd = P.sbuf(f"rstd{tag}", [128, TT], F32)
        self.brstd = P.buf(f"rstd{tag}")


def prenorm(P, C, nb, ps_idx, x_t, bx, gcols, bg, xn_t, bxn):
    P.op("act", lambda e: e.activation(out=nb.sq[:], in_=x_t[:], func=AF.Square),
         reads=[bx], writes=[nb.bsq])
    rstd_from_sumsq(P, C, nb.sq, nb.bsq, 8, ps_idx, nb.rtmp, nb.rstd, nb.brstd, TT, 1.0 / D)
    for c in range(8):
        P.op("dve", lambda e, c=c: e.scalar_tensor_tensor(
            out=xn_t[:, c, :], in0=x_t[:, c, :], scalar=gcols[:, c:c + 1], in1=nb.rstd[:],
            op0=ALU.mult, op1=ALU.mult), reads=[bx, nb.brstd, bg], writes=[bxn])


def out_evac(P, C, nb, ps, pb, dc, hT, bhT):
    P.op("dve", lambda e: e.tensor_copy(out=hT[:, dc, :], in_=ps[:, 0:TT]),
         reads=[C.psb[pb]], writes=[bhT[dc]])
    P.op("act", lambda e: e.activation(out=nb.sq[:, dc, :], in_=hT[:, dc, :], func=AF.Square),
         reads=[bhT[dc]], writes=[nb.bsq])


def postnorm_residual(P, C, nb, ps_idx, hT, bhT, x_t, bx, gcols, bg):
    rstd_from_sumsq(P, C, nb.sq, nb.bsq, 8, ps_idx, nb.rtmp, nb.rstd, nb.brstd, TT, 1.0 / D)
    for c in range(8):
        P.op("dve", lambda e, c=c: e.scalar_tensor_tensor(
            out=hT[:, c, :], in0=hT[:, c, :], scalar=gcols[:, c:c + 1], in1=nb.rstd[:],
            op0=ALU.mult, op1=ALU.mult), reads=[bhT[c], nb.brstd, bg], writes=[bhT[c]])
    P.op("pool", lambda e: e.tensor_tensor(out=x_t[:], in0=x_t[:], in1=hT[:], op=ALU.add),
         reads=list(bhT) + [bx], writes=[bx])


def emit_ffn(P, C, xs, xs_dst, xs_b, w_in, w_out, prm, nt=NT):
    win = P.sbuf("win", [128, 8, 2 * DFF], BF16)
    wout = P.sbuf("wout", [128, NFC, D], BF16)
    bwin = [[P.buf() for _ in range(4)] for _ in range(8)]
    bwout = P.bufs(NFC // 2, "wout")
    prm_t = P.sbuf("prm", [128, 16], F32)
    bprm = P.buf("prm")
    P.dma("sp", C.dsem(), prm_t[:], prm, writes=[bprm])
    hg = P.sbuf("hg", [128, 8], F32)
    bhg = P.buf("hg")
    P.op("dve", lambda e: e.tensor_scalar(out=hg[:], in0=prm_t[:, 8:16], scalar1=0.5, scalar2=None,
                                          op0=ALU.mult), reads=[bprm], writes=[bhg])
    wl = WLoader(P, C)
    w_in_v = w_in.rearrange("(kc p) n -> p kc n", p=128)
    QW = 2 * DFF // 4
    for q in range(4):
        for kc in range(8):
            wl.load(w_in_v[:, kc, q * QW:(q + 1) * QW], win[:, kc, q * QW:(q + 1) * QW], bwin[kc][q])
    w_out_v = w_out.rearrange("(fc p) n -> p fc n", p=128)
    for i in range(NFC // 2):
        wl.load(w_out_v[:, 2 * i:2 * i + 2, :], wout[:, 2 * i:2 * i + 2, :], bwout[i])

    xt = [P.sbuf(f"xt{i}", [128, 8, TT], F32) for i in range(2)]
    bxt = P.bufs(2, "xt")
    dxl = [C.dsem() for _ in range(2)]
    dxs = [C.dsem() for _ in range(2)]
    xn = [P.sbuf(f"xn{i}", [128, 8, TT], BF16) for i in range(2)]
    bxn = P.bufs(2, "xn")
    nb = NormBufs(P, "a")
    nb2 = NormBufs(P, "b", share=nb)
    sg = [P.sbuf(f"sg{i}", [128, TT], F32) for i in range(2)]
    bsg = P.bufs(2, "sg")
    act = P.sbuf("act", [128, NFC, TT], BF16)
    bact = P.bufs(NFC, "act")
    hT = P.sbuf("hT", [128, 8, TT], F32)
    bhT = P.bufs(8, "hT")

    def wbuf(kc, col):
        return bwin[kc][col // QW]

    for i in range(nt):
        s = i % 2
        x_t = xt[s]
        P.dma("sp", dxl[s], x_t[:], xs[i], reads=[xs_b[i]], writes=[bxt[s]])
        prenorm(P, C, nb, 0, x_t, bxt[s], prm_t[:, 0:8], bprm, xn[s], bxn[s])
        for j in range(NFC):
            pb = 1 + (j % 2)
            ps = C.ps[pb]
            for half, col0 in ((0, j * 128), (1, DFF + j * 128)):
                for kc in range(8):
                    P.op("pe", lambda e, ps=ps, half=half, col0=col0, kc=kc, s=s: e.matmul(
                        ps[:, half * TT:(half + 1) * TT], lhsT=win[:, kc, col0:col0 + 128],
                        rhs=xn[s][:, kc, :], start=(kc == 0), stop=(kc == 7)),
                        reads=[bxn[s], wbuf(kc, col0)], writes=[C.psb[pb]])
            g = sg[j % 2]
            P.op("act", lambda e, g=g, ps=ps: e.activation(out=g[:], in_=ps[:, 0:TT], func=AF.Silu),
                 reads=[C.psb[pb]], writes=[bsg[j % 2]])
            P.op("dve", lambda e, g=g, ps=ps, j=j: e.tensor_tensor(
                out=act[:, j, :], in0=ps[:, TT:2 * TT], in1=g[:], op=ALU.mult),
                reads=[C.psb[pb], bsg[j % 2]], writes=[bact[j]])
        for dc in range(8):
            pb = 3 + (dc % 4)
            ps = C.ps[pb]
            for fc in range(NFC):
                P.op("pe", lambda e, ps=ps, dc=dc, fc=fc: e.matmul(
                    ps[:, 0:TT], lhsT=wout[:, fc, dc * 128:(dc + 1) * 128],
                    rhs=act[:, fc, :], start=(fc == 0), stop=(fc == NFC - 1)),
                    reads=[bact[fc], bwout[fc // 2]], writes=[C.psb[pb]])
            out_evac(P, C, nb2, ps, pb, dc, hT, bhT)
        postnorm_residual(P, C, nb2, 7, hT, bhT, x_t, bxt[s], hg, bhg)
        P.dma("sp", dxs[s], xs_dst[i], x_t[:], reads=[bxt[s]], writes=[xs_b[i]])


REC_NPRM = 80


def emit_rec(P, C, xs, xs_dst, xs_b, w_in, w_a, w_x, w_out, prm, nt=NT):
    win = P.sbuf("rwin", [128, 8, 2 * D], BF16)
    wa = P.sbuf("rwa", [128, 8, 128], BF16)
    wx = P.sbuf("rwx", [128, 8, 128], BF16)
    wout = P.sbuf("rwout", [128, 8, D], BF16)
    bwin = P.bufs(8, "rwin")
    bwa = P.buf("rwa")
    bwx = P.buf("rwx")
    bwout = P.bufs(4, "rwout")
    prm_t = P.sbuf("rprm", [128, REC_NPRM], F32)
    bprm = P.buf("rprm")
    P.dma("sp", C.dsem(), prm_t[:], prm, writes=[bprm])
    wl = WLoader(P, C)
    w_in_v = w_in.rearrange("(kc p) n -> p kc n", p=128)
    for kc in range(8):
        wl.load(w_in_v[:, kc, :], win[:, kc, :], bwin[kc])
    wl.load(w_a.rearrange("n j k -> j n k"), wa[:], bwa)
    wl.load(w_x.rearrange("n j k -> j n k"), wx[:], bwx)
    w_out_v = w_out.rearrange("(kc p) n -> p kc n", p=128)
    for i in range(4):
        wl.load(w_out_v[:, 2 * i:2 * i + 2, :], wout[:, 2 * i:2 * i + 2, :], bwout[i])
    cc = P.sbuf("rcc", [128, 24], F32)
    bcc = P.buf("rcc")
    P.op("act", lambda e: e.activation(out=cc[:, 16:24], in_=prm_t[:, 72:80], func=AF.Exp, scale=-1.0),
         reads=[bprm], writes=[bcc])
    P.op("act", lambda e: e.activation(out=cc[:, 16:24], in_=cc[:, 16:24], func=AF.Ln, bias=1.0),
         reads=[bcc], writes=[bcc])
    P.op("dve", lambda e: e.tensor_scalar(out=cc[:, 0:8], in0=cc[:, 16:24], scalar1=-8.0, scalar2=None,
                                          op0=ALU.mult), reads=[bcc], writes=[bcc])
    P.op("dve", lambda e: e.tensor_scalar(out=cc[:, 8:16], in0=cc[:, 16:24], scalar1=-16.0, scalar2=None,
                                          op0=ALU.mult), reads=[bcc], writes=[bcc])

    xt = [P.sbuf(f"rxt{i}", [128, 8, TT], F32) for i in range(2)]
    bxt = P.bufs(2, "rxt")
    dxl = [C.dsem() for _ in range(2)]
    dxs = [C.dsem() for _ in range(2)]
    xn = [P.sbuf(f"rxn{i}", [128, 8, TT], BF16) for i in range(2)]
    bxn = P.bufs(2, "rxn")
    nb = NormBufs(P, "ra")
    nb2 = NormBufs(P, "rb", share=nb)
    gate = P.sbuf("rgate", [128, 8, TT], F32)
    bgate = P.bufs(8, "rgate")
    xraw = P.sbuf("rxraw", [128, 8, TT + 3], F32)
    bxraw = P.bufs(8, "rxraw")
    P.op("pool", lambda e: e.memset(xraw[:], 0.0), writes=bxraw)
    xc = P.sbuf("rxc", [128, 8, TT], F32)
    bxc = P.bufs(8, "rxc")
    xcb = P.sbuf("rxcb", [128, 8, TT], BF16)
    bxcb = P.bufs(8, "rxcb")
    rr = [P.sbuf(f"rr{i}", [128, TT], F32) for i in range(2)]
    ii = [P.sbuf(f"ri{i}", [128, TT], F32) for i in range(2)]
    aa = [P.sbuf(f"ra{i}", [128, TT], F32) for i in range(2)]
    a2 = [P.sbuf(f"ra2{i}", [128, TT], F32) for i in range(2)]
    uu = [P.sbuf(f"ru{i}", [128, TT], F32) for i in range(2)]
    brr = P.bufs(2, "rr")
    bii = P.bufs(2, "ri")
    baa = P.bufs(2, "ra")
    ba2 = P.bufs(2, "ra2")
    buu = P.bufs(2, "ru")
    hh = P.sbuf("rh", [128, 8, TT], F32)
    bhh = P.bufs(8, "rh")
    hst = P.sbuf("rhst", [128, 8], F32)
    bhst = P.bufs(8, "rhst")
    P.op("pool", lambda e: e.memset(hst[:], 0.0), writes=bhst)
    yb = P.sbuf("ryb", [128, 8, TT], BF16)
    byb = P.bufs(8, "ryb")
    hT = P.sbuf("rhT", [128, 8, TT], F32)
    bhT = P.bufs(8, "rhT")

    for i in range(nt):
        s = i % 2
        x_t = xt[s]
        P.dma("sp", dxl[s], x_t[:], xs[i], reads=[xs_b[i]], writes=[bxt[s]])
        prenorm(P, C, nb, 0, x_t, bxt[s], prm_t[:, 0:8], bprm, xn[s], bxn[s])
        for oc in range(16):
            pb = 1 + (oc % 2)
            ps = C.ps[pb]
            for kc in range(8):
                P.op("pe", lambda e, ps=ps, oc=oc, kc=kc, s=s: e.matmul(
                    ps[:, 0:TT], lhsT=win[:, kc, oc * 128:(oc + 1) * 128], rhs=xn[s][:, kc, :],
                    start=(kc == 0), stop=(kc == 7)), reads=[bxn[s], bwin[kc]], writes=[C.psb[pb]])
            if oc < 8:
                P.op("act", lambda e, ps=ps, oc=oc: e.activation(
                    out=gate[:, oc, :], in_=ps[:, 0:TT], func=AF.Gelu_apprx_tanh),
                    reads=[C.psb[pb]], writes=[bgate[oc]])
            else:
                c = oc - 8
                P.op("act", lambda e, ps=ps, c=c: e.copy(out=xraw[:, c, 3:3 + TT], in_=ps[:, 0:TT]),
                     reads=[C.psb[pb]], writes=[bxraw[c]])
        import os
        RSTOP = int(os.environ.get("REC_STOP", "99"))
        if RSTOP <= 1:
            P.dma("sp", dxs[s], xs_dst[i], x_t[:], reads=[bxt[s]] + bgate + bxraw, writes=[xs_b[i]])
            continue
        for c in range(8):
            P.op("act", lambda e, c=c: e.activation(
                out=xc[:, c, :], in_=xraw[:, c, 0:TT], func=AF.Identity,
                scale=prm_t[:, 16 + c:17 + c], bias=prm_t[:, 48 + c:49 + c]),
                reads=[bxraw[c], bprm], writes=[bxc[c]])
            for k in range(1, 4):
                P.op("dve", lambda e, c=c, k=k: e.scalar_tensor_tensor(
                    out=xc[:, c, :], in0=xraw[:, c, k:k + TT], scalar=prm_t[:, 16 + k * 8 + c:17 + k * 8 + c],
                    in1=xc[:, c, :], op0=ALU.mult, op1=ALU.add),
                    reads=[bxraw[c], bprm, bxc[c]], writes=[bxc[c]])
            P.op("pool", lambda e, c=c: e.tensor_copy(out=xraw[:, c, 0:3], in_=xraw[:, c, TT:TT + 3]),
                 reads=[bxraw[c]], writes=[bxraw[c]])
            P.op("pool", lambda e, c=c: e.tensor_copy(out=xcb[:, c, :], in_=xc[:, c, :]),
                 reads=[bxc[c]], writes=[bxcb[c]])
        if RSTOP <= 2:
            P.dma("sp", dxs[s], xs_dst[i], x_t[:], reads=[bxt[s]] + bxcb + bxraw, writes=[xs_b[i]])
            continue
        for n in range(8):
            k2 = n % 2
            pb = 3 + (n % 2) * 2
            psr = C.ps[pb]
            psi = C.ps[pb + 1]
            P.op("pe", lambda e, psr=psr, n=n: e.matmul(psr[:, 0:TT], lhsT=wa[:, n, :], rhs=xcb[:, n, :],
                                                        start=True, stop=True),
                 reads=[bxcb[n], bwa], writes=[C.psb[pb]])
            P.op("pe", lambda e, psi=psi, n=n: e.matmul(psi[:, 0:TT], lhsT=wx[:, n, :], rhs=xcb[:, n, :],
                                                        start=True, stop=True),
                 reads=[bxcb[n], bwx], writes=[C.psb[pb + 1]])
            P.op("act", lambda e, psr=psr, n=n, k2=k2: e.activation(
                out=rr[k2][:], in_=psr[:, 0:TT], func=AF.Sigmoid, bias=prm_t[:, 56 + n:57 + n]),
                reads=[C.psb[pb], bprm], writes=[brr[k2]])
            P.op("act", lambda e, psi=psi, n=n, k2=k2: e.activation(
                out=ii[k2][:], in_=psi[:, 0:TT], func=AF.Sigmoid, bias=prm_t[:, 64 + n:65 + n]),
                reads=[C.psb[pb + 1], bprm], writes=[bii[k2]])
            P.op("act", lambda e, n=n, k2=k2: e.activation(
                out=aa[k2][:], in_=rr[k2][:], func=AF.Exp, scale=cc[:, n:n + 1]),
                reads=[brr[k2], bcc], writes=[baa[k2]])
            P.op("act", lambda e, n=n, k2=k2: e.activation(
                out=a2[k2][:], in_=rr[k2][:], func=AF.Exp, scale=cc[:, 8 + n:9 + n]),
                reads=[brr[k2], bcc], writes=[ba2[k2]])
            P.op("act", lambda e, k2=k2: e.activation(
                out=a2[k2][:], in_=a2[k2][:], func=AF.Sqrt, scale=-1.0, bias=C.onec[:]),
                reads=[ba2[k2], C.bconst], writes=[ba2[k2]])
            P.op("dve", lambda e, n=n, k2=k2: e.tensor_tensor(
                out=uu[k2][:], in0=ii[k2][:], in1=xc[:, n, :], op=ALU.mult),
                reads=[bii[k2], bxc[n]], writes=[buu[k2]])
            P.op("dve", lambda e, k2=k2: e.tensor_tensor(
                out=uu[k2][:], in0=uu[k2][:], in1=a2[k2][:], op=ALU.mult),
                reads=[buu[k2], ba2[k2]], writes=[buu[k2]])
            P.op("dve", lambda e, n=n, k2=k2: e.tensor_tensor_scan(
                out=hh[:, n, :], data0=aa[k2][:], data1=uu[k2][:], initial=hst[:, n:n + 1],
                op0=ALU.mult, op1=ALU.add), reads=[baa[k2], buu[k2], bhst[n]], writes=[bhh[n]])
            P.op("pool", lambda e, n=n: e.tensor_copy(out=hst[:, n:n + 1], in_=hh[:, n, TT - 1:TT]),
                 reads=[bhh[n]], writes=[bhst[n]])
            P.op("dve", lambda e, n=n: e.tensor_tensor(
                out=yb[:, n, :], in0=hh[:, n, :], in1=gate[:, n, :], op=ALU.mult),
                reads=[bhh[n], bgate[n]], writes=[byb[n]])
        if RSTOP <= 3:
            P.dma("sp", dxs[s], xs_dst[i], x_t[:], reads=[bxt[s]] + byb + bhst, writes=[xs_b[i]])
            continue
        for dc in range(8):
            pb = 1 + (dc % 2)
            ps = C.ps[pb]
            for kc in range(8):
                P.op("pe", lambda e, ps=ps, dc=dc, kc=kc: e.matmul(
                    ps[:, 0:TT], lhsT=wout[:, kc, dc * 128:(dc + 1) * 128], rhs=yb[:, kc, :],
                    start=(kc == 0), stop=(kc == 7)), reads=[byb[kc], bwout[kc // 2]], writes=[C.psb[pb]])
            out_evac(P, C, nb2, ps, pb, dc, hT, bhT)
        postnorm_residual(P, C, nb2, 7, hT, bhT, x_t, bxt[s], prm_t[:, 8:16], bprm)
        P.dma("sp", dxs[s], xs_dst[i], x_t[:], reads=[bxt[s]], writes=[xs_b[i]])


def to_blocked(x_seq):
    return np.ascontiguousarray(x_seq.reshape(NT, TT, 8, 128).transpose(0, 3, 2, 1))


def from_blocked(xb):
    return np.ascontiguousarray(xb.transpose(0, 3, 2, 1).reshape(S, D))


def cols(v):
    return np.ascontiguousarray(v.reshape(-1, 128).T)


def make_ctx(P):
    C = Ctx(P)
    C.epsc = P.sbuf("epsc", [128, 1], F32, glob=True)
    P.op("pool", lambda e: e.memset(C.epsc[:], EPS), writes=[C.bconst])
    C.onec = P.sbuf("onec", [128, 1], F32, glob=True)
    P.op("pool", lambda e: e.memset(C.onec[:], 1.0), writes=[C.bconst])
    return C


def build_ffn_prog(nt=NT):
    P = Prog()
    C = make_ctx(P)
    xs_in = P.dram("xs", [NT, 128, 8, TT], F32, "ExternalInput")
    xs_out = P.dram("xs_out", [NT, 128, 8, TT], F32, "ExternalOutput")
    w_in = P.dram("w_in", [D, 2 * DFF], F32, "ExternalInput")
    w_out = P.dram("w_out", [DFF, D], F32, "ExternalInput")
    prm = P.dram("prm", [128, 16], F32, "ExternalInput")
    xs_b = P.bufs(NT, "xs")
    emit_ffn(P, C, xs_in, xs_out, xs_b, w_in, w_out, prm, nt)
    P.flush(final=True)
    return P


def build_rec_prog(nt=NT):
    P = Prog()
    C = make_ctx(P)
    xs_in = P.dram("xs", [NT, 128, 8, TT], F32, "ExternalInput")
    xs_out = P.dram("xs_out", [NT, 128, 8, TT], F32, "ExternalOutput")
    w_in = P.dram("w_in", [D, 2 * D], F32, "ExternalInput")
    w_a = P.dram("w_a", [8, 128, 128], F32, "ExternalInput")
    w_x = P.dram("w_x", [8, 128, 128], F32, "ExternalInput")
    w_out = P.dram("w_out", [D, D], F32, "ExternalInput")
    prm = P.dram("prm", [128, REC_NPRM], F32, "ExternalInput")
    xs_b = P.bufs(NT, "xs")
    emit_rec(P, C, xs_in, xs_out, xs_b, w_in, w_a, w_x, w_out, prm, nt)
    P.flush(final=True)
    return P


def rec_prm(g_pre, g_post, conv_w, conv_b, b_a, b_x, lam):
    return np.ascontiguousarray(np.concatenate(
        [cols(g_pre), cols(g_post)] + [cols(conv_w[k]) for k in range(4)] +
        [cols(conv_b), cols(b_a), cols(b_x), cols(lam)], axis=1).astype(np.float32))


HYB_IN = 5656
HYB_NPRM = 76
HYB_NROW = 1080
QSCALE = 128 ** -0.5
NEG = -30000.0


def hyb_consts(P, C):
    if hasattr(C, "tri_f"):
        return
    C.tri_f = P.sbuf("tri_f", [128, 128], F32, glob=True)
    C.ones_f = P.sbuf("ones_f", [128, 128], F32, glob=True)
    C.ident_f = P.sbuf("ident_f", [128, 128], F32, glob=True)
    C.negmask = P.sbuf("negmask", [128, 128], BF16, glob=True)
    w = [C.bconst]
    idx = P.sbuf("idx_f", [128, 128], F32)
    P.op("pool", lambda e: e.iota(idx[:], [[1, 128]], base=0, channel_multiplier=-1,
                                  allow_small_or_imprecise_dtypes=True), writes=w)
    P.op("dve", lambda e: e.tensor_scalar(out=C.tri_f[:], in0=idx[:], scalar1=0.0, scalar2=None,
                                          op0=ALU.is_ge), reads=w, writes=w)
    P.op("dve", lambda e: e.tensor_scalar(out=C.ident_f[:], in0=idx[:], scalar1=0.0, scalar2=None,
                                          op0=ALU.is_equal), reads=w, writes=w)
    P.op("dve", lambda e: e.tensor_scalar(out=C.negmask[:], in0=idx[:], scalar1=0.0, scalar2=NEG,
                                          op0=ALU.is_lt, op1=ALU.mult), reads=w, writes=w)
    P.op("pool", lambda e: e.memset(C.ones_f[:], 1.0), writes=w)


def hyb_scratch(P):
    return dict(
        qT=P.dram("h_qT", [8, 128, S], BF16),
        kT=P.dram("h_kT", [8, 128, S], BF16),
        v=P.dram("h_v", [S // 128, 128, 1024], BF16),
        pccol=P.dram("h_pccol", [S // 128, 128, 8], F32),
        pcrow=P.dram("h_pcrow", [8, S], F32),
        yT=P.dram("h_yT", [16, 128, S], BF16),
    )


def emit_hyb_a(P, C, xs, xs_b, w_in, prm, rowp, scr, nt=NT):
    hyb_consts(P, C)
    win = P.sbuf("hwin", [128, 8, HYB_IN], BF16)
    NQ = 6
    QW = [1024, 1024, 1024, 1024, 1024, HYB_IN - 5120]
    bwin = [[P.buf() for _ in range(NQ)] for _ in range(8)]
    prm_t = P.sbuf("hprm", [128, HYB_NPRM], F32)
    bprm = P.buf("hprm")
    P.dma("sp", C.dsem(), prm_t[:], prm, writes=[bprm])
    row_t = P.sbuf("hrow", [128, HYB_NROW], F32)
    brow = P.buf("hrow")
    P.dma("sp", C.dsem(), row_t[:], rowp[0:1, :].to_broadcast([128, HYB_NROW]), writes=[brow])
    P.op("act", lambda e: e.activation(out=row_t[:, 16:32], in_=row_t[:, 16:32], func=AF.Exp),
         reads=[brow], writes=[brow])
    P.op("dve", lambda e: e.tensor_scalar(out=row_t[:, 16:32], in0=row_t[:, 16:32], scalar1=-1.0,
                                          scalar2=None, op0=ALU.mult), reads=[brow], writes=[brow])
    dtb_row = row_t[:, 0:16]
    a_row = row_t[:, 16:32]
    dsk_row = row_t[:, 32:48]
    fb_row = row_t[:, 48:56]
    gn_row = row_t[:, 56:1080]
    wl = WLoader(P, C, n=2, width=1024)
    w_in_v = w_in.rearrange("(kc p) n -> p kc n", p=128)
    for q in range(NQ):
        for kc in range(8):
            c0 = q * 1024
            wl.load(w_in_v[:, kc, c0:c0 + QW[q]], win[:, kc, c0:c0 + QW[q]], bwin[kc][q])

    def wb(kc, col):
        return bwin[kc][min(col // 1024, NQ - 1)]

    def wbs(kc, c0, c1):
        a = min(c0 // 1024, NQ - 1)
        b = min((c1 - 1) // 1024, NQ - 1)
        return [bwin[kc][k] for k in range(a, b + 1)]

    xt = P.sbuf("hxt", [128, 8, TT], F32)
    bxt = P.buf("hxt")
    dxl = C.dsem()
    xn = [P.sbuf(f"hxn{i}", [128, 8, TT], BF16) for i in range(2)]
    bxn = P.bufs(2, "hxn")
    nb = NormBufs(P, "ha")
    xraw = P.sbuf("hxraw", [128, 12, TT + 3], F32)
    bxraw = P.bufs(12, "hxraw")
    P.op("pool", lambda e: e.memset(xraw[:], 0.0), writes=bxraw)
    xcv = [P.sbuf(f"hxcv{i}", [128, TT], F32) for i in range(2)]
    bxcv = P.bufs(2, "hxcv")
    xbc = P.sbuf("hxbc", [128, 12, TT], BF16)
    bxbc = P.bufs(12, "hxbc")
    qk = P.sbuf("hqk", [128, 16, TT], BF16)
    bqk = P.bufs(2, "hqk")
    dqk = [C.dsem(), C.dsem()]
    zs = P.sbuf("hzs", [128, 1024], F32)
    bzs = P.buf("hzs")
    vst = P.sbuf("hvst", [128, 1024], BF16)
    bvst = P.buf("hvst")
    dv = C.dsem()
    t24 = P.sbuf("ht24", [128, 24], F32)
    sp24 = P.sbuf("hsp24", [128, 24], F32)
    r24 = P.sbuf("hr24", [128, 24], F32)
    b24 = P.buf("h24")
    acs = P.sbuf("hacs", [128, 24], F32)
    tot = P.sbuf("htot", [128, 24], F32)
    e48 = P.sbuf("he48", [128, 48], F32)
    nacs = P.sbuf("hnacs", [128, 16], F32)
    w2 = P.sbuf("hw2", [128, 16], F32)
    bsm = P.buf("hsmall")
    carry = P.sbuf("hcarry", [128, 8], F32)
    bcarry = P.buf("hcarry")
    P.op("pool", lambda e: e.memset(carry[:], 0.0), writes=[bcarry])
    pcc = P.sbuf("hpcc", [128, 8], F32)
    bpcc = P.buf("hpcc")
    dpcc = C.dsem()
    pcr = P.sbuf("hpcr", [8, 128], F32)
    bpcr = P.buf("hpcr")
    dpcr = C.dsem()
    xtok = P.sbuf("hxtok", [128, 1024], BF16)
    bxtok = P.buf("hxtok")
    btok = P.sbuf("hbtok", [128, 256], BF16)
    bbtok = P.buf("hbtok")
    X = P.sbuf("hX", [128, 1024], BF16)
    bX = P.buf("hX")
    Xd = P.sbuf("hXd", [128, 1024], BF16)
    bXd = P.buf("hXd")
    cb = P.sbuf("hcb", [128, 2, 128], F32)
    bcb = P.buf("hcb")
    Lt = [P.sbuf(f"hLt{i}", [128, 4, 128], F32) for i in range(2)]
    bLt = P.bufs(2, "hLt")
    M = P.sbuf("hM", [128, 16, 128], BF16)
    bM = P.bufs(4, "hM")
    yoff = P.sbuf("hyoff", [128, 1024], F32)
    byoff = P.buf("hyoff")
    yy = P.sbuf("hyy", [128, 1024], F32)
    byy = P.buf("hyy")
    tmp = P.sbuf("htmp", [128, 1024], F32)
    btmp = P.buf("htmp")
    junk = P.sbuf("hjunk", [128, 512], BF16)
    bjunk = P.buf("hjunk")
    ss = P.sbuf("hss", [128, 4], F32)
    bss = P.buf("hss")
    St = P.sbuf("hS", [128, 1024], F32)
    bS = P.buf("hS")
    Sbf = P.sbuf("hSbf", [128, 1024], BF16)
    bSbf = P.buf("hSbf")
    P.op("pool", lambda e: e.memset(St[:], 0.0), writes=[bS])
    P.op("pool", lambda e: e.memset(Sbf[:], 0.0), writes=[bSbf])
    ytok = P.sbuf("hytok", [128, 1024], BF16)
    bytok = P.buf("hytok")
    yTs = P.sbuf("hyTs", [128, 8, 128], BF16)
    byTs = P.buf("hyTs")
    dyT = C.dsem()
    ps = C.ps
    psb = C.psb
    ps6b = ps[6][:, :].bitcast(BF16)
    ps7b = ps[7][:, :].bitcast(BF16)

    fm_cols = [1024 + 128 * j for j in range(12)] + [2576 + 128 * j for j in range(8)] + \
              [3600 + 128 * j for j in range(8)]

    for i in range(nt):
        s = i % 2
        P.dma("sp", dxl, xt[:], xs[i], reads=[xs_b[i]], writes=[bxt])
        prenorm(P, C, nb, 0, xt, bxt, prm_t[:, 0:8], bprm, xn[s], bxn[s])
        for idx, col0 in enumerate(fm_cols):
            pb = 1 + idx % 2
            for kc in range(8):
                P.op("pe", lambda e, pb=pb, col0=col0, kc=kc, s=s: e.matmul(
                    ps[pb][:, 0:TT], lhsT=win[:, kc, col0:col0 + 128], rhs=xn[s][:, kc, :],
                    start=(kc == 0), stop=(kc == 7)),
                    reads=[bxn[s]] + wbs(kc, col0, col0 + 128), writes=[psb[pb]])
            if idx < 12:
                P.op("act", lambda e, pb=pb, idx=idx: e.copy(out=xraw[:, idx, 3:3 + TT], in_=ps[pb][:, 0:TT]),
                     reads=[psb[pb]], writes=[bxraw[idx]])
            elif idx < 20:
                P.op("act", lambda e, pb=pb, idx=idx: e.activation(
                    out=qk[:, idx - 12, :], in_=ps[pb][:, 0:TT], func=AF.Identity, scale=QSCALE),
                    reads=[psb[pb]], writes=[bqk[0]])
            else:
                P.op("dve", lambda e, pb=pb, idx=idx: e.tensor_copy(out=qk[:, idx - 12, :], in_=ps[pb][:, 0:TT]),
                     reads=[psb[pb]], writes=[bqk[1]])
        tsl = slice(i * TT, (i + 1) * TT)
        P.dma("sp", dqk[0], scr["qT"].rearrange("h d t -> d h t")[:, :, tsl], qk[:, 0:8, :], reads=[bqk[0]])
        P.dma("sp", dqk[1], scr["kT"].rearrange("h d t -> d h t")[:, :, tsl], qk[:, 8:16, :], reads=[bqk[1]])
        for j in range(12):
            cv = xcv[j % 2]
            bcv = bxcv[j % 2]
            P.op("act", lambda e, j=j, cv=cv: e.activation(
                out=cv[:], in_=xraw[:, j, 0:TT], func=AF.Identity,
                scale=prm_t[:, 16 + j:17 + j], bias=prm_t[:, 64 + j:65 + j]),
                reads=[bxraw[j], bprm], writes=[bcv])
            for k in range(1, 4):
                P.op("dve", lambda e, j=j, k=k, cv=cv: e.scalar_tensor_tensor(
                    out=cv[:], in0=xraw[:, j, k:k + TT], scalar=prm_t[:, 16 + k * 12 + j:17 + k * 12 + j],
                    in1=cv[:], op0=ALU.mult, op1=ALU.add), reads=[bxraw[j], bprm, bcv], writes=[bcv])
            P.op("pool", lambda e, j=j: e.tensor_copy(out=xraw[:, j, 0:3], in_=xraw[:, j, TT:TT + 3]),
                 reads=[bxraw[j]], writes=[bxraw[j]])
            P.op("act", lambda e, j=j, cv=cv: e.activation(out=xbc[:, j, :], in_=cv[:], func=AF.Silu),
                 reads=[bcv], writes=[bxbc[j]])
        for cc in range(2):
            cs = slice(cc * 128, (cc + 1) * 128)
            ch = i * 2 + cc
            for hf in range(2):
                pb = 3 + hf
                for kc in range(8):
                    P.op("pe", lambda e, pb=pb, hf=hf, kc=kc, s=s, cs=cs: e.matmul(
                        ps[pb][:, :], lhsT=xn[s][:, kc, cs], rhs=win[:, kc, hf * 512:(hf + 1) * 512],
                        start=(kc == 0), stop=(kc == 7)),
                        reads=[bxn[s]] + wbs(kc, hf * 512, hf * 512 + 512), writes=[psb[pb]])
                P.op("act", lambda e, pb=pb, hf=hf: e.activation(
                    out=zs[:, hf * 512:(hf + 1) * 512], in_=ps[pb][:, :], func=AF.Silu),
                    reads=[psb[pb]], writes=[bzs])
            for hf in range(2):
                pb = 3 + hf
                c0 = 4624 + hf * 512
                for kc in range(8):
                    P.op("pe", lambda e, pb=pb, c0=c0, kc=kc, s=s, cs=cs: e.matmul(
                        ps[pb][:, :], lhsT=xn[s][:, kc, cs], rhs=win[:, kc, c0:c0 + 512],
                        start=(kc == 0), stop=(kc == 7)),
                        reads=[bxn[s]] + wbs(kc, c0, c0 + 512), writes=[psb[pb]])
                P.op("dve", lambda e, pb=pb, hf=hf: e.tensor_copy(
                    out=vst[:, hf * 512:(hf + 1) * 512], in_=ps[pb][:, :]),
                    reads=[psb[pb]], writes=[bvst])
            P.dma("sp", dv, scr["v"][ch], vst[:], reads=[bvst])
            for (o0, o1, c0) in ((0, 16, 2560), (16, 24, 5648)):
                for kc in range(8):
                    P.op("pe", lambda e, o0=o0, o1=o1, c0=c0, kc=kc, s=s, cs=cs: e.matmul(
                        ps[5][:, o0:o1], lhsT=xn[s][:, kc, cs], rhs=win[:, kc, c0:c0 + (o1 - o0)],
                        start=(kc == 0), stop=(kc == 7)),
                        reads=[bxn[s]] + wbs(kc, c0, c0 + 16), writes=[psb[5]])
            P.op("dve", lambda e: e.tensor_tensor(out=t24[:, 0:16], in0=ps[5][:, 0:16], in1=dtb_row, op=ALU.add),
                 reads=[psb[5], brow], writes=[b24])
            P.op("dve", lambda e: e.scalar_tensor_tensor(
                out=t24[:, 16:24], in0=ps[5][:, 16:24], scalar=-1.0, in1=fb_row, op0=ALU.mult, op1=ALU.subtract),
                reads=[psb[5], brow], writes=[b24])
            P.op("act", lambda e: e.activation(out=t24[:], in_=t24[:], func=AF.Exp), reads=[b24], writes=[b24])
            P.op("act", lambda e: e.activation(out=sp24[:], in_=t24[:], func=AF.Ln, bias=1.0),
                 reads=[b24], writes=[b24])
            P.op("dve", lambda e: e.tensor_tensor(out=r24[:, 0:16], in0=sp24[:, 0:16], in1=a_row, op=ALU.mult),
                 reads=[b24, brow], writes=[b24])
            P.op("dve", lambda e: e.tensor_copy(out=r24[:, 16:24], in_=sp24[:, 16:24]), reads=[b24], writes=[b24])
            P.op("pe", lambda e: e.matmul(ps[5][:, 32:56], lhsT=C.tri_f[:], rhs=r24[:], start=True, stop=True),
                 reads=[b24, C.bconst], writes=[psb[5]])
            P.op("pe", lambda e: e.matmul(ps[5][:, 64:88], lhsT=C.ones_f[:], rhs=r24[:], start=True, stop=True),
                 reads=[b24, C.bconst], writes=[psb[5]])
            P.op("dve", lambda e: e.tensor_copy(out=acs[:], in_=ps[5][:, 32:56]), reads=[psb[5]], writes=[bsm])
            P.op("dve", lambda e: e.tensor_copy(out=tot[:], in_=ps[5][:, 64:88]), reads=[psb[5]], writes=[bsm])
            P.op("dve", lambda e: e.tensor_tensor(out=pcc[:], in0=acs[:, 16:24], in1=carry[:], op=ALU.add),
                 reads=[bsm, bcarry], writes=[bpcc])
            P.op("dve", lambda e: e.tensor_tensor(out=carry[:], in0=carry[:], in1=tot[:, 16:24], op=ALU.add),
                 reads=[bsm, bcarry], writes=[bcarry])
            P.dma("sp", dpcc, scr["pccol"][ch], pcc[:], reads=[bpcc])
            P.op("pe", lambda e: e.transpose(out=ps[5][0:8, 128:256], in_=pcc[:], identity=C.ident_f[:]),
                 reads=[bpcc, C.bconst], writes=[psb[5]])
            P.op("dve", lambda e: e.tensor_copy(out=pcr[:], in_=ps[5][0:8, 128:256]), reads=[psb[5]], writes=[bpcr])
            P.dma("sp", dpcr, scr["pcrow"][:, ch * 128:(ch + 1) * 128], pcr[:], reads=[bpcr])
            P.op("dve", lambda e: e.tensor_copy(out=e48[:, 0:16], in_=acs[:, 0:16]), reads=[bsm], writes=[bsm])
            P.op("dve", lambda e: e.tensor_tensor(out=e48[:, 16:32], in0=tot[:, 0:16], in1=acs[:, 0:16],
                                                  op=ALU.subtract), reads=[bsm], writes=[bsm])
            P.op("dve", lambda e: e.tensor_copy(out=e48[:, 32:48], in_=tot[:, 0:16]), reads=[bsm], writes=[bsm])
            P.op("act", lambda e: e.activation(out=e48[:], in_=e48[:], func=AF.Exp), reads=[bsm], writes=[bsm])
            P.op("dve", lambda e: e.tensor_scalar(out=nacs[:], in0=acs[:, 0:16], scalar1=-1.0, scalar2=None,
                                                  op0=ALU.mult), reads=[bsm], writes=[bsm])
            P.op("dve", lambda e: e.tensor_tensor(out=w2[:], in0=sp24[:, 0:16], in1=e48[:, 16:32], op=ALU.mult),
                 reads=[bsm, b24], writes=[bsm])
            eacs = e48[:, 0:16]
            etot = e48[:, 32:48]
            for j in range(8):
                P.op("pe", lambda e, j=j, cs=cs: e.transpose(
                    out=ps6b[:, j * 128:(j + 1) * 128], in_=xbc[:, j, cs], identity=C.ident[:]),
                    reads=[bxbc[j], C.bconst], writes=[psb[6]])
            P.op("act", lambda e: e.copy(out=xtok[:], in_=ps6b[:, :]), reads=[psb[6]], writes=[bxtok])
            for g in range(2):
                P.op("pe", lambda e, g=g, cs=cs: e.transpose(
                    out=ps7b[:, g * 128:(g + 1) * 128], in_=xbc[:, 8 + g, cs], identity=C.ident[:]),
                    reads=[bxbc[8 + g], C.bconst], writes=[psb[7]])
            P.op("act", lambda e: e.copy(out=btok[:], in_=ps7b[:, 0:256]), reads=[psb[7]], writes=[bbtok])
            xtok3 = xtok[:, :].rearrange("p (h d) -> p h d", h=16)
            P.op("dve", lambda e: e.tensor_tensor(
                out=X[:, :].rearrange("p (h d) -> p h d", h=16), in0=xtok3,
                in1=sp24[:, 0:16].unsqueeze(2).to_broadcast([128, 16, 64]), op=ALU.mult),
                reads=[bxtok, b24], writes=[bX])
            P.op("dve", lambda e: e.tensor_tensor(
                out=Xd[:, :].rearrange("p (h d) -> p h d", h=16), in0=xtok3,
                in1=w2[:, :].unsqueeze(2).to_broadcast([128, 16, 64]), op=ALU.mult),
                reads=[bxtok, bsm], writes=[bXd])
            for g in range(2):
                P.op("pe", lambda e, g=g, cs=cs: e.matmul(
                    ps[5][:, 256 + g * 128:384 + g * 128], lhsT=xbc[:, 8 + g, cs], rhs=xbc[:, 10 + g, cs],
                    start=True, stop=True), reads=[bxbc[8 + g], bxbc[10 + g]], writes=[psb[5]])
            P.op("dve", lambda e: e.tensor_copy(out=cb[:, :, :].rearrange("p g l -> p (g l)"),
                                                in_=ps[5][:, 256:512]), reads=[psb[5]], writes=[bcb])
            for g in range(2):
                P.op("pe", lambda e, g=g, cs=cs: e.matmul(
                    ps[3 + g][:, :], lhsT=xbc[:, 10 + g, cs], rhs=Sbf[:, g * 512:(g + 1) * 512],
                    start=True, stop=True), reads=[bxbc[10 + g], bSbf], writes=[psb[3 + g]])
                P.op("dve", lambda e, g=g: e.tensor_tensor(
                    out=yoff[:, g * 512:(g + 1) * 512].rearrange("p (h d) -> p h d", h=8),
                    in0=ps[3 + g][:, :].rearrange("p (h d) -> p h d", h=8),
                    in1=eacs[:, g * 8:(g + 1) * 8].unsqueeze(2).to_broadcast([128, 8, 64]), op=ALU.mult),
                    reads=[psb[3 + g], bsm], writes=[byoff])
            for g in range(2):
                P.op("pe", lambda e, g=g: e.matmul(
                    ps[3 + g][:, :], lhsT=btok[:, g * 128:(g + 1) * 128], rhs=Xd[:, g * 512:(g + 1) * 512],
                    start=True, stop=True), reads=[bbtok, bXd], writes=[psb[3 + g]])
                Sg = St[:, g * 512:(g + 1) * 512]
                P.op("dve", lambda e, g=g, Sg=Sg: e.tensor_tensor(
                    out=Sg.rearrange("p (h d) -> p h d", h=8), in0=Sg.rearrange("p (h d) -> p h d", h=8),
                    in1=etot[:, g * 8:(g + 1) * 8].unsqueeze(2).to_broadcast([128, 8, 64]), op=ALU.mult),
                    reads=[bS, bsm], writes=[bS])
                P.op("dve", lambda e, g=g, Sg=Sg: e.tensor_tensor(out=Sg, in0=Sg, in1=ps[3 + g][:, :], op=ALU.add),
                     reads=[bS, psb[3 + g]], writes=[bS])
            P.op("pool", lambda e: e.tensor_copy(out=Sbf[:], in_=St[:]), reads=[bS], writes=[bSbf])
            for hq in range(4):
                g = hq // 2
                pbk = 6 + hq % 2
                L = Lt[hq % 2]
                for h4 in range(4):
                    h = hq * 4 + h4
                    P.op("pe", lambda e, pbk=pbk, h4=h4, h=h: e.matmul(
                        ps[pbk][:, h4 * 128:(h4 + 1) * 128], lhsT=r24[:, h:h + 1].to_broadcast([128, 128]),
                        rhs=C.tri_f[:], start=True, stop=False), reads=[b24, C.bconst], writes=[psb[pbk]])
                    P.op("pe", lambda e, pbk=pbk, h4=h4: e.matmul(
                        ps[pbk][:, h4 * 128:(h4 + 1) * 128], lhsT=C.ident[:], rhs=C.negmask[:],
                        start=False, stop=True), reads=[C.bconst], writes=[psb[pbk]])
                for h4 in range(4):
                    h = hq * 4 + h4
                    P.op("dve", lambda e, pbk=pbk, h4=h4, h=h, L=L: e.tensor_scalar(
                        out=L[:, h4, :], in0=ps[pbk][:, h4 * 128:(h4 + 1) * 128], scalar1=nacs[:, h:h + 1],
                        scalar2=None, op0=ALU.add), reads=[psb[pbk], bsm], writes=[bLt[hq % 2]])
                P.op("act", lambda e, L=L: e.activation(out=L[:], in_=L[:], func=AF.Exp),
                     reads=[bLt[hq % 2]], writes=[bLt[hq % 2]])
                P.op("dve", lambda e, L=L, hq=hq, g=g: e.tensor_tensor(
                    out=M[:, hq * 4:(hq + 1) * 4, :], in0=L[:],
                    in1=cb[:, g:g + 1, :].to_broadcast([128, 4, 128]), op=ALU.mult),
                    reads=[bLt[hq % 2], bcb], writes=[bM[hq]])
                for h4 in range(4):
                    h = hq * 4 + h4
                    P.op("pe", lambda e, g=g, h=h: e.matmul(
                        ps[1 + g][:, (h % 8) * 64:(h % 8) * 64 + 64], lhsT=M[:, h, :],
                        rhs=X[:, h * 64:(h + 1) * 64], start=True, stop=True),
                        reads=[bM[hq], bX], writes=[psb[1 + g]])
            for g in range(2):
                P.op("dve", lambda e, g=g: e.tensor_tensor(
                    out=yy[:, g * 512:(g + 1) * 512], in0=ps[1 + g][:, :], in1=yoff[:, g * 512:(g + 1) * 512],
                    op=ALU.add), reads=[psb[1 + g], byoff], writes=[byy])
            P.op("dve", lambda e: e.tensor_tensor(
                out=tmp[:, :].rearrange("p (h d) -> p h d", h=16), in0=xtok3,
                in1=dsk_row.unsqueeze(2).to_broadcast([128, 16, 64]), op=ALU.mult),
                reads=[bxtok, brow], writes=[btmp])
            P.op("pool", lambda e: e.tensor_tensor(out=yy[:], in0=yy[:], in1=tmp[:], op=ALU.add),
                 reads=[byy, btmp], writes=[byy])
            P.op("dve", lambda e: e.tensor_tensor(out=yy[:], in0=yy[:], in1=zs[:], op=ALU.mult),
                 reads=[byy, bzs], writes=[byy])
            for g in range(2):
                P.op("act", lambda e, g=g: e.activation(
                    out=junk[:], in_=yy[:, g * 512:(g + 1) * 512], func=AF.Square, accum_out=ss[:, g:g + 1]),
                    reads=[byy], writes=[bjunk, bss])
            P.op("act", lambda e: e.activation(out=ss[:, 2:4], in_=ss[:, 0:2], func=AF.Sqrt, scale=1.0 / 512,
                                               bias=C.epsc[:]), reads=[bss, C.bconst], writes=[bss])
            P.op("dve", lambda e: e.reciprocal(out=ss[:, 2:4], in_=ss[:, 2:4]), reads=[bss], writes=[bss])
            for g in range(2):
                P.op("dve", lambda e, g=g: e.scalar_tensor_tensor(
                    out=ytok[:, g * 512:(g + 1) * 512], in0=yy[:, g * 512:(g + 1) * 512],
                    scalar=ss[:, 2 + g:3 + g], in1=gn_row[:, g * 512:(g + 1) * 512], op0=ALU.mult, op1=ALU.mult),
                    reads=[byy, bss, brow], writes=[bytok])
            for j in range(8):
                P.op("pe", lambda e, j=j: e.transpose(
                    out=ps6b[:, j * 128:(j + 1) * 128], in_=ytok[:, j * 128:(j + 1) * 128], identity=C.ident[:]),
                    reads=[bytok, C.bconst], writes=[psb[6]])
            P.op("act", lambda e: e.copy(out=yTs[:, :, :].rearrange("p c t -> p (c t)"), in_=ps6b[:, :]),
                 reads=[psb[6]], writes=[byTs])
            P.dma("sp", dyT, scr["yT"].rearrange("c p t -> p c t")[:, 0:8, ch * 128:(ch + 1) * 128], yTs[:],
                  reads=[byTs])


def att_qk(P, C, c, s, t, q0, h, KT, QT, PR, T, bT, Pt, bPt, pcc, bpcc, bh):
    ps, psb = C.ps, C.psb
    diag = c >= 4 * t
    lo = (c - 4 * t) * 128 if diag else 0
    pbk = 1 + c % 2
    P.op("pe", lambda e: e.matmul(ps[pbk][:, lo:512], lhsT=KT[s][:, c * 128:(c + 1) * 128],
                                  rhs=QT[s][:, q0 + lo:q0 + 512], start=True, stop=not diag),
         reads=[bh[s]], writes=[psb[pbk]])
    if diag:
        P.op("pe", lambda e: e.matmul(ps[pbk][:, lo:lo + 128], lhsT=C.ident[:], rhs=C.negmask[:],
                                      start=False, stop=True), reads=[C.bconst], writes=[psb[pbk]])
    Tt = T[c % 2]
    P.op("dve", lambda e: e.tensor_tensor(out=Tt[:, lo:512], in0=ps[pbk][:, lo:512],
                                          in1=PR[s][:, q0 + lo:q0 + 512], op=ALU.subtract),
         reads=[psb[pbk], bh[s]], writes=[bT[c % 2]])
    Pp = Pt[c % 3]
    P.op("act", lambda e: e.activation(out=Pp[:, lo:512], in_=Tt[:, lo:512], func=AF.Exp,
                                       bias=pcc[:, c, h:h + 1]),
         reads=[bT[c % 2], bpcc], writes=[bPt[c % 3]])


def att_pv(P, C, c, s, t, po, pd, nck, V, Pt, bPt, bh):
    ps, psb = C.ps, C.psb
    diag = c >= 4 * t
    lo = (c - 4 * t) * 128 if diag else 0
    Pp = Pt[c % 3]
    P.op("pe", lambda e: e.matmul(ps[po][:, lo:512], lhsT=V[s][:, c, :], rhs=Pp[:, lo:512],
                                  start=(c == 0), stop=(c == nck - 1)),
         reads=[bPt[c % 3], bh[s]], writes=[psb[po]])
    P.op("pe", lambda e: e.matmul(ps[pd][:, lo:512], lhsT=C.ones[:], rhs=Pp[:, lo:512],
                                  start=(c == 0), stop=(c == nck - 1)),
         reads=[bPt[c % 3], C.bconst], writes=[psb[pd]])


def emit_hyb_b(P, C, scr, nq=8, nheads=8):
    hyb_consts(P, C)
    ps = C.ps
    psb = C.psb
    NCH = S // 128
    pcc = P.sbuf("apcc", [128, NCH, 8], F32)
    bpcc = P.buf("apcc")
    L = nq * 512
    LC = L // 128
    P.dma("sp", C.dsem(), pcc[:, 0:LC, :], scr["pccol"].rearrange("c p h -> p c h")[:, 0:LC, :], writes=[bpcc])
    QT = [P.sbuf(f"aQT{i}", [128, S], BF16) for i in range(2)]
    KT = [P.sbuf(f"aKT{i}", [128, S], BF16) for i in range(2)]
    V = [P.sbuf(f"aV{i}", [128, NCH, 128], BF16) for i in range(2)]
    PR = [P.sbuf(f"aPR{i}", [128, S], F32) for i in range(2)]
    bh = P.bufs(2, "ahead")
    dh = [[C.dsem() for _ in range(4)] for _ in range(2)]
    T = [P.sbuf(f"aT{i}", [128, 512], F32) for i in range(2)]
    bT = P.bufs(2, "aT")
    Pt = [P.sbuf(f"aPt{i}", [128, 512], BF16) for i in range(3)]
    bPt = P.bufs(3, "aPt")
    rden = P.sbuf("arden", [128, 512], F32)
    brden = P.buf("arden")
    oT = [P.sbuf(f"aoT{i}", [128, 512], BF16) for i in range(2)]
    boT = P.bufs(2, "aoT")
    doT = [C.dsem(), C.dsem()]
    n_out = 0
    for h in range(nheads):
        s = h % 2
        P.dma("sp", dh[s][0], QT[s][:, 0:L], scr["qT"][h, :, 0:L], writes=[bh[s]])
        P.dma("sp", dh[s][1], KT[s][:, 0:L], scr["kT"][h, :, 0:L], writes=[bh[s]])
        P.dma("sp", dh[s][2], V[s][:, 0:LC, :],
              scr["v"].rearrange("c p f -> p c f")[:, 0:LC, h * 128:(h + 1) * 128], writes=[bh[s]])
        P.dma("sp", dh[s][3], PR[s][:, 0:L], scr["pcrow"][h:h + 1, 0:L].to_broadcast([128, L]), writes=[bh[s]])
        for t in range(nq):
            q0 = t * 512
            nck = 4 * t + 4
            po = 3 + (t % 2)
            pd = 5 + (t % 2)

            for c in range(-1, nck):
                if c + 1 < nck:
                    att_qk(P, C, c + 1, s, t, q0, h, KT, QT, PR, T, bT, Pt, bPt, pcc, bpcc, bh)
                if c >= 0:
                    att_pv(P, C, c, s, t, po, pd, nck, V, Pt, bPt, bh)
            o = oT[n_out % 2]
            bo = boT[n_out % 2]
            P.op("dve", lambda e, pd=pd: e.reciprocal(out=rden[:], in_=ps[pd][:, :]), reads=[psb[pd]], writes=[brden])
            P.op("dve", lambda e, o=o, po=po: e.tensor_tensor(out=o[:], in0=ps[po][:, :], in1=rden[:], op=ALU.mult),
                 reads=[psb[po], brden], writes=[bo])
            P.dma("sp", doT[n_out % 2], scr["yT"][8 + h, :, q0:q0 + 512], o[:], reads=[bo])
            n_out += 1


def emit_hyb_c(P, C, xs, xs_dst, xs_b, w_out, prm, scr, nt=NT):
    wout = P.sbuf("cwout", [128, 16, D], BF16)
    bwout = P.bufs(8, "cwout")
    prm_t = P.sbuf("cprm", [128, HYB_NPRM], F32)
    bprm = P.buf("cprm")
    P.dma("sp", C.dsem(), prm_t[:], prm, writes=[bprm])
    wl = WLoader(P, C)
    w_out_v = w_out.rearrange("(kc p) n -> p kc n", p=128)
    for i in range(8):
        wl.load(w_out_v[:, 2 * i:2 * i + 2, :], wout[:, 2 * i:2 * i + 2, :], bwout[i])
    xt = [P.sbuf(f"cxt{i}", [128, 8, TT], F32) for i in range(2)]
    bxt = P.bufs(2, "cxt")
    dxl = [C.dsem() for _ in range(2)]
    dxs = [C.dsem() for _ in range(2)]
    yt = [P.sbuf(f"cyt{i}", [128, 16, TT], BF16) for i in range(2)]
    byt = P.bufs(2, "cyt")
    dyl = [C.dsem() for _ in range(2)]
    nb2 = NormBufs(P, "cb")
    hT = P.sbuf("chT", [128, 8, TT], F32)
    bhT = P.bufs(8, "chT")
    yv = scr["yT"].rearrange("c p t -> p c t")
    for i in range(nt):
        s = i % 2
        x_t = xt[s]
        P.dma("sp", dxl[s], x_t[:], xs[i], reads=[xs_b[i]], writes=[bxt[s]])
        P.dma("sp", dyl[s], yt[s][:], yv[:, :, i * TT:(i + 1) * TT], writes=[byt[s]])
        for dc in range(8):
            pb = 1 + (dc % 2)
            ps = C.ps[pb]
            for kc in range(16):
                P.op("pe", lambda e, ps=ps, dc=dc, kc=kc, s=s: e.matmul(
                    ps[:, 0:TT], lhsT=wout[:, kc, dc * 128:(dc + 1) * 128], rhs=yt[s][:, kc, :],
                    start=(kc == 0), stop=(kc == 15)), reads=[byt[s], bwout[kc // 2]], writes=[C.psb[pb]])
            out_evac(P, C, nb2, ps, pb, dc, hT, bhT)
        postnorm_residual(P, C, nb2, 7, hT, bhT, x_t, bxt[s], prm_t[:, 8:16], bprm)
        P.dma("sp", dxs[s], xs_dst[i], x_t[:], reads=[bxt[s]], writes=[xs_b[i]])


def hyb_prm(g_pre, g_post, conv_w, conv_b):
    return np.ascontiguousarray(np.concatenate(
        [cols(g_pre), cols(g_post)] + [cols(conv_w[k]) for k in range(4)] + [cols(conv_b)], axis=1)
        .astype(np.float32))


def hyb_rowp(dt_bias, a_log, d_skip, fox_b, norm_g):
    return np.ascontiguousarray(np.concatenate([dt_bias, a_log, d_skip, fox_b, norm_g])[None, :]
                                .astype(np.float32))


def build_hyb_prog(nt=NT, nq=8, nheads=8):
    P = Prog()
    C = make_ctx(P)
    xs_in = P.dram("xs", [NT, 128, 8, TT], F32, "ExternalInput")
    xs_out = P.dram("xs_out", [NT, 128, 8, TT], F32, "ExternalOutput")
    w_in = P.dram("w_in", [D, HYB_IN], F32, "ExternalInput")
    w_out = P.dram("w_out", [2 * D, D], F32, "ExternalInput")
    prm = P.dram("prm", [128, HYB_NPRM], F32, "ExternalInput")
    rowp = P.dram("rowp", [1, HYB_NROW], F32, "ExternalInput")
    scr = hyb_scratch(P)
    xs_b = P.bufs(NT, "xs")
    emit_hyb_a(P, C, xs_in, xs_b, w_in, prm, rowp, scr, nt)
    P.flush()
    emit_hyb_b(P, C, scr, nq, nheads)
    P.flush()
    emit_hyb_c(P, C, xs_in, xs_out, xs_b, w_out, prm, scr, nt)
    P.flush(final=True)
    return P, scr


DEPTH = 4
_TEST_HOOK = None


def build_full(nt=NT, depth=DEPTH):
    P = Prog()
    C = make_ctx(P)
    xs_in = P.dram("xs", [NT, 128, 8, TT], F32, "ExternalInput")
    xs_out = P.dram("xs_out", [NT, 128, 8, TT], F32, "ExternalOutput")
    xs_b = P.bufs(NT, "xs")
    scr = hyb_scratch(P)
    src = xs_in

    def ffn(l, j):
        nonlocal src
        w_in = P.dram(f"ffn_w_in_{l}_{j}", [D, 2 * DFF], F32, "ExternalInput")
        w_out = P.dram(f"ffn_w_out_{l}_{j}", [DFF, D], F32, "ExternalInput")
        prm = P.dram(f"ffn_prm_{l}_{j}", [128, 16], F32, "ExternalInput")
        emit_ffn(P, C, src, xs_out, xs_b, w_in, w_out, prm, nt)
        P.flush()
        src = xs_out

    for l in range(depth):
        ffn(l, 0)
        if l % 2 == 0:
            i = l // 2
            w_in = P.dram(f"hyb_w_in_{i}", [D, HYB_IN], F32, "ExternalInput")
            w_out = P.dram(f"hyb_w_out_{i}", [2 * D, D], F32, "ExternalInput")
            prm = P.dram(f"hyb_prm_{i}", [128, HYB_NPRM], F32, "ExternalInput")
            rowp = P.dram(f"hyb_rowp_{i}", [1, HYB_NROW], F32, "ExternalInput")
            emit_hyb_a(P, C, src, xs_b, w_in, prm, rowp, scr, nt)
            P.flush()
            emit_hyb_b(P, C, scr, nt * TT // 512)
            P.flush()
            emit_hyb_c(P, C, src, xs_out, xs_b, w_out, prm, scr, nt)
            P.flush()
        else:
            j = l // 2
            w_in = P.dram(f"rec_w_in_{j}", [D, 2 * D], F32, "ExternalInput")
            w_a = P.dram(f"rec_w_a_{j}", [8, 128, 128], F32, "ExternalInput")
            w_x = P.dram(f"rec_w_x_{j}", [8, 128, 128], F32, "ExternalInput")
            w_out = P.dram(f"rec_w_out_{j}", [D, D], F32, "ExternalInput")
            prm = P.dram(f"rec_prm_{j}", [128, REC_NPRM], F32, "ExternalInput")
            emit_rec(P, C, src, xs_out, xs_b, w_in, w_a, w_x, w_out, prm, nt)
            P.flush()
        ffn(l, 1)
    P.flush(final=True)
    return P


def kernel(x, norm_g, ffn_w_in, ffn_w_out, hyb_w_in, ssm_conv_w, ssm_conv_b, ssm_dt_bias,
           ssm_a_log, ssm_d, ssm_norm_g, fox_b_f, hyb_w_out, rec_w_in, rec_conv_w, rec_conv_b,
           rec_w_a, rec_b_a, rec_w_x, rec_b_x, rec_lambda, rec_w_out):
    f32 = np.float32
    A = lambda a: np.ascontiguousarray(np.asarray(a, dtype=f32))
    x = A(x)
    norm_g = A(norm_g)
    shared = {}
    for l in range(DEPTH):
        for j in range(2):
            shared[f"ffn_w_in_{l}_{j}"] = A(ffn_w_in[l, j])
            shared[f"ffn_w_out_{l}_{j}"] = A(ffn_w_out[l, j])
            ga, gb = (norm_g[l, 0], norm_g[l, 1]) if j == 0 else (norm_g[l, 4], norm_g[l, 5])
            shared[f"ffn_prm_{l}_{j}"] = np.ascontiguousarray(np.concatenate([cols(ga), cols(gb)], axis=1))
        if l % 2 == 0:
            i = l // 2
            shared[f"hyb_w_in_{i}"] = A(hyb_w_in[i])
            shared[f"hyb_w_out_{i}"] = A(hyb_w_out[i])
            shared[f"hyb_prm_{i}"] = hyb_prm(norm_g[l, 2], norm_g[l, 3], A(ssm_conv_w[i]), A(ssm_conv_b[i]))
            shared[f"hyb_rowp_{i}"] = hyb_rowp(A(ssm_dt_bias[i]), A(ssm_a_log[i]), A(ssm_d[i]),
                                               A(fox_b_f[i]), A(ssm_norm_g[i]))
        else:
            j = l // 2
            shared[f"rec_w_in_{j}"] = A(rec_w_in[j])
            shared[f"rec_w_a_{j}"] = A(rec_w_a[j])
            shared[f"rec_w_x_{j}"] = A(rec_w_x[j])
            shared[f"rec_w_out_{j}"] = A(rec_w_out[j])
            shared[f"rec_prm_{j}"] = rec_prm(norm_g[l, 2], norm_g[l, 3], A(rec_conv_w[j]), A(rec_conv_b[j]),
                                            A(rec_b_a[j]), A(rec_b_x[j]), A(rec_lambda[j]))
    if _TEST_HOOK is not None:
        return _TEST_HOOK(shared, x)
    P = build_full()
    n = x.shape[0]
    in_maps = []
    for b in range(n):
        m = dict(shared)
        m["xs"] = to_blocked(x[b])
        in_maps.append(m)
    res = run_bass_kernel_spmd(P.nc, in_maps, core_ids=list(range(n)))
    out = np.stack([from_blocked(np.asarray(r["xs_out"])) for r in res.results], axis=0)
    return out.astype(f32)
```
